# Optimizing a Trainium2 kernel written in Bass

```python
import jax, jax.numpy as jnp
from jax import lax
import numpy as np

D_MODEL = 1024
BATCH = 16
SEQ = 4096
DEPTH = 4

N_A_LAYERS = DEPTH // 2
N_B_LAYERS = DEPTH - N_A_LAYERS

CHUNK = 128
A_WIDTH = 2 * D_MODEL
A_GROUPS = 8
A_GROUP_CH = A_WIDTH // A_GROUPS

HEAD_DIM = 64
N_Q_HEADS = D_MODEL // HEAD_DIM
N_KV_HEADS = 4
Q_PER_KV = N_Q_HEADS // N_KV_HEADS
B_WIDTH = N_Q_HEADS * HEAD_DIM
KV_WIDTH = N_KV_HEADS * HEAD_DIM
WINDOW = 128
BLOCK = 128

EPS = 1e-6

kernel_name = "yoco_gmlp_swa_sink_hybrid"


def rms_norm(x, g):
    xf = x.astype(jnp.float32)
    y = xf * lax.rsqrt(jnp.mean(xf * xf, axis=-1, keepdims=True) + EPS)
    return (y * g.astype(jnp.float32)).astype(x.dtype)


def layer_norm(x, g, b):
    xf = x.astype(jnp.float32)
    mu = jnp.mean(xf, axis=-1, keepdims=True)
    var = jnp.mean(jnp.square(xf - mu), axis=-1, keepdims=True)
    y = (xf - mu) * lax.rsqrt(var + EPS)
    return (y * g.astype(jnp.float32) + b.astype(jnp.float32)).astype(x.dtype)


def alibi_slopes(n_heads):
    return jnp.asarray(np.array([2.0 ** (-8.0 * (h + 1) / n_heads) for h in range(n_heads)], dtype=np.float32))


def gmlp_mixer(h, w_in, ln_g, ln_b, w_s, b_s, w_out):
    bsz, s_len, _ = h.shape
    u, v, z = jnp.split(h @ w_in, 3, axis=-1)
    u = jax.nn.gelu(u)
    v = layer_norm(jax.nn.gelu(v), ln_g, ln_b)
    n_chunks = s_len // CHUNK
    v = v.reshape(bsz, n_chunks, CHUNK, A_GROUPS, A_GROUP_CH)
    causal = jnp.tril(jnp.ones((CHUNK, CHUNK), dtype=bool))
    w_causal = jnp.where(causal[None], w_s, 0.0).astype(v.dtype)
    s = jnp.einsum('gts,bnsgc->bntgc', w_causal, v) + b_s.T[:, :, None].astype(v.dtype)
    s = s.reshape(bsz, s_len, A_WIDTH)
    y = u * s * jax.nn.silu(z)
    return y @ w_out


def swa_mixer(h, k, v, w_in, sinks, w_out):
    bsz, s_len, _ = h.shape
    q, z = jnp.split(h @ w_in, 2, axis=-1)
    q = q * (HEAD_DIM ** -0.5)
    n_blocks = s_len // BLOCK
    q_blocks = q.reshape(bsz, n_blocks, BLOCK, N_KV_HEADS, Q_PER_KV, HEAD_DIM).transpose(1, 0, 2, 3, 4, 5)
    pad = jnp.zeros((bsz, BLOCK, N_KV_HEADS, HEAD_DIM), k.dtype)
    k_pad = jnp.concatenate([pad, k], axis=1)
    v_pad = jnp.concatenate([pad, v], axis=1)

    r = jnp.arange(BLOCK)[:, None]
    c = jnp.arange(2 * BLOCK)[None, :]
    dist = BLOCK + r - c
    in_window = (dist >= 0) & (dist < WINDOW)
    slopes = alibi_slopes(N_Q_HEADS).reshape(N_KV_HEADS, Q_PER_KV)
    alibi = -slopes[:, :, None, None] * dist.astype(jnp.float32)
    sink = sinks.astype(jnp.float32).reshape(N_KV_HEADS, Q_PER_KV)[None, :, :, None]

    def block_fn(args):
        i, qb = args
        start = i * BLOCK
        kb = lax.dynamic_slice_in_dim(k_pad, start, 2 * BLOCK, axis=1)
        vb = lax.dynamic_slice_in_dim(v_pad, start, 2 * BLOCK, axis=1)
        logits = jnp.einsum('btkgd,bskd->bkgts', qb, kb).astype(jnp.float32) + alibi
        valid = in_window & ((start - BLOCK + c) >= 0)
        logits = jnp.where(valid, logits, -jnp.inf)
        m = jnp.maximum(jnp.max(logits, axis=-1), sink)
        p = jnp.exp(logits - m[..., None])
        denom = jnp.sum(p, axis=-1) + jnp.exp(sink - m)
        o = jnp.einsum('bkgts,bskd->btkgd', p.astype(vb.dtype), vb).astype(jnp.float32)
        o = o / denom.transpose(0, 3, 1, 2)[..., None]
        return o.astype(h.dtype)

    out = lax.map(block_fn, (jnp.arange(n_blocks), q_blocks))
    out = out.transpose(1, 0, 2, 3, 4, 5).reshape(bsz, s_len, B_WIDTH)
    y = out * jax.nn.silu(z)
    return y @ w_out


def setup_inputs(seed: int = 0) -> dict:
    key = jax.random.key(seed)
    ks = jax.random.split(key, 20)
    f32 = jnp.float32
    nrm = lambda k, shape, scale: jax.random.normal(k, shape, f32) * scale
    x = jax.random.normal(ks[0], (BATCH, SEQ, D_MODEL), f32)
    a_w_in = nrm(ks[1], (N_A_LAYERS, D_MODEL, 3 * A_WIDTH), D_MODEL ** -0.5)
    a_ln_g = 1.0 + nrm(ks[2], (N_A_LAYERS, A_WIDTH), 0.05)
    a_ln_b = nrm(ks[3], (N_A_LAYERS, A_WIDTH), 0.02)
    a_w_s = nrm(ks[4], (N_A_LAYERS, A_GROUPS, CHUNK, CHUNK), CHUNK ** -0.5)
    a_b_s = 1.0 + nrm(ks[5], (N_A_LAYERS, A_GROUPS, CHUNK), 0.1)
    a_w_out = nrm(ks[6], (N_A_LAYERS, A_WIDTH, D_MODEL), A_WIDTH ** -0.5)
    a_pre_g = 1.0 + nrm(ks[7], (N_A_LAYERS, D_MODEL), 0.05)
    a_post_g = 1.0 + nrm(ks[8], (N_A_LAYERS, D_MODEL), 0.05)
    kv_norm_g = 1.0 + nrm(ks[9], (D_MODEL,), 0.05)
    w_k = nrm(ks[10], (D_MODEL, KV_WIDTH), D_MODEL ** -0.5)
    w_v = nrm(ks[11], (D_MODEL, KV_WIDTH), D_MODEL ** -0.5)
    b_w_in = nrm(ks[12], (N_B_LAYERS, D_MODEL, 2 * B_WIDTH), D_MODEL ** -0.5)
    b_sinks = nrm(ks[13], (N_B_LAYERS, N_Q_HEADS), 0.5)
    b_w_out = nrm(ks[14], (N_B_LAYERS, B_WIDTH, D_MODEL), B_WIDTH ** -0.5)
    b_pre_g = 1.0 + nrm(ks[15], (N_B_LAYERS, D_MODEL), 0.05)
    b_post_g = 1.0 + nrm(ks[16], (N_B_LAYERS, D_MODEL), 0.05)
    return {"x": x, "a_w_in": a_w_in, "a_ln_g": a_ln_g, "a_ln_b": a_ln_b, "a_w_s": a_w_s,
            "a_b_s": a_b_s, "a_w_out": a_w_out, "a_pre_g": a_pre_g, "a_post_g": a_post_g,
            "kv_norm_g": kv_norm_g, "w_k": w_k, "w_v": w_v, "b_w_in": b_w_in, "b_sinks": b_sinks,
            "b_w_out": b_w_out, "b_pre_g": b_pre_g, "b_post_g": b_post_g}


def reference(x, a_w_in, a_ln_g, a_ln_b, a_w_s, a_b_s, a_w_out, a_pre_g, a_post_g,
              kv_norm_g, w_k, w_v, b_w_in, b_sinks, b_w_out, b_pre_g, b_post_g):
    bsz, s_len, _ = x.shape
    h = x
    k_shared = None
    v_shared = None
    for layer in range(DEPTH):
        if layer < N_A_LAYERS:
            i = layer
            y = gmlp_mixer(rms_norm(h, a_pre_g[i]), a_w_in[i], a_ln_g[i], a_ln_b[i],
                           a_w_s[i], a_b_s[i], a_w_out[i])
            h = h + rms_norm(y, a_post_g[i])
            if layer == N_A_LAYERS - 1:
                kv_in = rms_norm(h, kv_norm_g)
                k_shared = (kv_in @ w_k).reshape(bsz, s_len, N_KV_HEADS, HEAD_DIM)
                v_shared = (kv_in @ w_v).reshape(bsz, s_len, N_KV_HEADS, HEAD_DIM)
        else:
            j = layer - N_A_LAYERS
            y = swa_mixer(rms_norm(h, b_pre_g[j]), k_shared, v_shared,
                          b_w_in[j], b_sinks[j], b_w_out[j])
            h = h + rms_norm(y, b_post_g[j])
    return h
```

```python
import numpy as np
from contextlib import ExitStack
import concourse.bass as bass
import concourse.mybir as mybir
from concourse.bass_utils import run_bass_kernel_spmd

F32 = mybir.dt.float32
BF16 = mybir.dt.bfloat16
AF = mybir.ActivationFunctionType
ALU = mybir.AluOpType
AX = mybir.AxisListType

D = 1024
NCORES = 8
TT = 512
NB = 4
EPS = 1e-6
NSLOT = 6
INTERLEAVE_WOUT = False
SLABW = 4096
COMPUTE = ("pe", "act", "dve", "pool")


def I(name, *a, **k):
    return lambda e: getattr(e, name)(*a, **k)


class Op:
    __slots__ = ("eng", "fn", "dma", "eidx", "deps", "mark", "semval", "tok")


class Prog:
    def __init__(self):
        self.eops = {e: [] for e in COMPUTE + ("sp",)}
        self.lw = {}
        self.rd = {}
        self.const = set()
        self.dma_cnt = {}
        self.last_dma = {}
        self.pending = {}

    def barrier(self, skip_prefix="cast"):
        deps = []
        for e in COMPUTE:
            for o in reversed(self.eops[e]):
                if o.dma is None:
                    deps.append(o)
                    break
        for k, o in self.last_dma.items():
            if not k.startswith(skip_prefix):
                deps.append(o)
        for e in self.eops:
            self.pending[e] = list(deps)

    def op(self, eng, fn, r=(), w=(), dma=None):
        o = Op()
        o.eng, o.fn, o.dma, o.mark, o.semval, o.tok = eng, fn, dma, False, 0, None
        o.eidx = len(self.eops[eng])
        self.eops[eng].append(o)
        deps = {}

        def add(p, raw):
            if p is None or p is o:
                return
            key = ("d", p.dma) if p.dma is not None else ("e", p.eng)
            cur = deps.get(key)
            if p.dma is not None:
                if cur is None or p.tok[1] > cur[0].tok[1]:
                    deps[key] = (p, False)
            else:
                if cur is None or p.eidx > cur[0].eidx:
                    deps[key] = (p, raw)
                elif p.eidx == cur[0].eidx and raw:
                    deps[key] = (p, True)

        if eng in self.pending:
            for p_ in self.pending.pop(eng):
                add(p_, True)
        for k in r:
            add(self.lw.get(k), True)
        for k in w:
            add(self.lw.get(k), False)
            for p in self.rd.get(k, ()):
                add(p, False)
        for k in r:
            if k in self.const:
                continue
            lst = self.rd.setdefault(k, [])
            skey = ("d", dma) if dma is not None else ("e", eng)
            lst[:] = [p for p in lst if (("d", p.dma) if p.dma is not None else ("e", p.eng)) != skey]
            lst.append(o)
        for k in w:
            self.lw[k] = o
            self.rd[k] = []
        if dma is not None:
            c = self.dma_cnt.get(dma, 0) + 1
            self.dma_cnt[dma] = c
            o.tok = (dma, 16 * c)
            self.last_dma[dma] = o
        o.deps = list(deps.values())
        return o

    def _needs_wait(self, o, p, raw):
        if p.dma is not None:
            return True
        if p.eng != o.eng or o.dma is not None:
            return True
        return o.eng != "pe"

    def finalize(self):
        for e, lst in self.eops.items():
            for o in lst:
                for p, raw in o.deps:
                    if p.dma is None and self._needs_wait(o, p, raw):
                        p.mark = True
        for e in COMPUTE:
            c = 0
            for o in self.eops[e]:
                if o.mark and o.dma is None:
                    c += 1
                    o.semval = c

    def emit_engine(self, e, eng, esem, dsem):
        waited = {}
        for o in self.eops[e]:
            for p, raw in o.deps:
                if not self._needs_wait(o, p, raw):
                    continue
                if p.dma is not None:
                    sk, val = ("d", p.tok[0]), p.tok[1]
                    sem = dsem[p.tok[0]]
                else:
                    sk, val = ("e", p.eng), p.semval
                    sem = esem[p.eng]
                if waited.get(sk, 0) >= val:
                    continue
                eng.wait_ge(sem, val)
                waited[sk] = val
            ins = o.fn(eng)
            if o.dma is not None:
                ins.then_inc(dsem[o.dma], 16)
            elif o.mark:
                ins.then_inc(esem[e], 1)


def slab_plan():
    seq = []
    for l in range(2):
        seq += [("av", l, i) for i in range(4)] + [("au", l, i) for i in range(4)]
        seq += [("az", l, i) for i in range(4)] + [("ao", l, i) for i in range(4)]
    for j in range(2):
        seq += [("bq", j, i) for i in range(2)] + [("bz", j, i) for i in range(2)]
        if j == 0:
            seq += [("wk", 0, 0), ("wv", 0, 0)]
        seq += [("bo", j, i) for i in range(2)]
    return seq


def build_nc(nseq, seqlen):
    assert seqlen % TT == 0
    tiles_per_seq = seqlen // TT
    ntiles = nseq * tiles_per_seq
    plan = slab_plan()
    nslab = len(plan)
    slab_id = {s: i for i, s in enumerate(plan)}

    nc = bass.Bass("TRN2", target_bir_lowering=False)
    dt = nc.dram_tensor
    x = dt("x", [nseq, seqlen, D], F32, kind="ExternalInput").ap()
    a_w_in = dt("a_w_in", [2, D, 6144], F32, kind="ExternalInput").ap()
    a_ln_g = dt("a_ln_g", [2, 2048], F32, kind="ExternalInput").ap()
    a_ln_b = dt("a_ln_b", [2, 2048], F32, kind="ExternalInput").ap()
    a_w_s = dt("a_w_s", [2, 8, 128, 128], F32, kind="ExternalInput").ap()
    a_b_s = dt("a_b_s", [2, 8, 128], F32, kind="ExternalInput").ap()
    a_w_out = dt("a_w_out", [2, 2048, D], F32, kind="ExternalInput").ap()
    a_pre_g = dt("a_pre_g", [2, D], F32, kind="ExternalInput").ap()
    a_post_g = dt("a_post_g", [2, D], F32, kind="ExternalInput").ap()
    kv_norm_g = dt("kv_norm_g", [1, D], F32, kind="ExternalInput").ap()
    w_k = dt("w_k", [D, 256], F32, kind="ExternalInput").ap()
    w_v = dt("w_v", [D, 256], F32, kind="ExternalInput").ap()
    b_w_in = dt("b_w_in", [2, D, 2048], F32, kind="ExternalInput").ap()
    b_sinks = dt("b_sinks", [1, 32], F32, kind="ExternalInput").ap()
    b_w_out = dt("b_w_out", [2, D, D], F32, kind="ExternalInput").ap()
    b_pre_g = dt("b_pre_g", [2, D], F32, kind="ExternalInput").ap()
    b_post_g = dt("b_post_g", [2, D], F32, kind="ExternalInput").ap()
    c_ident = dt("c_ident", [128, 128], F32, kind="ExternalInput").ap()
    c_mask = dt("c_mask", [128, 128], F32, kind="ExternalInput").ap()
    c_etab = dt("c_etab", [128, 8 * 512], F32, kind="ExternalInput").ap()
    out = dt("out", [nseq, seqlen, D], F32, kind="ExternalOutput").ap()
    wbf = dt("wbf", [nslab, 128, SLABW], BF16, kind="Internal").ap()

    P = Prog()
    es = ExitStack()
    with es:
        def sb(name, shape, dtype):
            return es.enter_context(nc.sbuf_tensor(name, shape, dtype))

        h = sb("h", [128, NB, D], F32)
        xnT = sb("xnT", [128, 8, TT], BF16)
        bufU = sb("bufU", [128, 16, TT], BF16)
        bufVZ = sb("bufVZ", [128, 16 * TT], BF16)
        vbuf = bufVZ[:, :].rearrange("p (b c) -> p b c", b=NB)
        zT = bufVZ[:, :].rearrange("p (c t) -> p c t", c=16)
        xs = [sb(f"xs{i}", [128, D], BF16) for i in range(4)]
        tmpb = [sb(f"tmp{i}", [128, D], F32) for i in range(2)]
        t1b = [sb(f"t1b{i}", [128, TT], BF16) for i in range(2)]
        junk = sb("junk", [128, 2048], BF16)
        gpre = sb("gpre", [128, D], F32)
        gkv = sb("gkv", [128, D], F32)
        gpost = sb("gpost", [128, D], F32)
        bias2 = sb("bias2", [128, 32, 128], BF16)
        WcT = sb("WcT", [128, 16, 128], BF16)
        lng = sb("lng", [128, 32], F32)
        lnb = sb("lnb", [128, 32], F32)
        kT2 = sb("kT2", [128, 4, 8 * 128], BF16)
        Vaug = sb("Vaug", [128, 8, 4, 192], BF16)
        etab = sb("etab", [128, 8, 512], BF16)
        pexp = [sb(f"pexp{i}", [128, 512], BF16) for i in range(4)]
        pT = [sb(f"pT{i}", [128, 512], BF16) for i in range(8)]
        lnd = [sb(f"lnd{i}", [128, 256], F32) for i in range(2)]
        rden = [sb(f"rden{i}", [128, 256], F32) for i in range(2)]
        rz = [sb(f"rz{i}", [128, 256], F32) for i in range(2)]
        ring = [sb(f"ring{i}", [128, SLABW], BF16) for i in range(NSLOT)]
        esrow = sb("esrow", [1, 16, 256], BF16)
        sel = sb("sel", [1, 2, 128], BF16)
        stat = sb("stat", [128, 256], F32)
        ident = sb("ident", [128, 128], BF16)
        ones = sb("ones", [128, 128], BF16)
        mhalf = sb("mhalf", [128, 1], F32)
        epst = sb("epst", [128, 1], F32)
        sink32 = sb("sink32", [1, 32], F32)
        psum = es.enter_context(nc.psum_tensor("psum", [128, 8, 512], F32))
        banks = [psum[:, i, :] for i in range(8)]

        esem = {e: es.enter_context(nc.semaphore(f"s_{e}")) for e in COMPUTE}
        dma_keys = ([f"ring{i}" for i in range(NSLOT)] + [f"hl{b}" for b in range(NB)]
                    + [f"hs{b}" for b in range(NB)] + ["gpre", "gkv", "gpost", "xe0", "xe1", "xe2", "xe3"]
                    + [f"cast{i}" for i in range(8)] + ["setup"] + [f"su{i}" for i in range(4)])
        dsem = {k: es.enter_context(nc.semaphore(f"d_{k}")) for k in dma_keys}

        state = {"bank": 0, "col": 0, "rot": {}}

        def newbank():
            i = state["bank"] % 8
            state["bank"] += 1
            return i

        def newbank_att():
            if not INTERLEAVE_WOUT:
                return newbank()
            i = state.get("att", 0) % 6
            state["att"] = state.get("att", 0) + 1
            return i

        def newbank2():
            if state["bank"] % 2:
                state["bank"] += 1
            i = state["bank"] % 8
            state["bank"] += 2
            return i, i + 1

        def newcol(n=1):
            c = state["col"]
            if c + n > 256:
                c = 0
            state["col"] = c + n
            return c

        def rot(name, n):
            i = state["rot"].get(name, 0)
            state["rot"][name] = i + 1
            return i % n

        def stk(c, n=1):
            return [("st", c + i) for i in range(n)]

        ci = 0
        for si, (kind, l, i) in enumerate(plan):
            dsts, srcs = [], []
            wv3 = wbf[si]
            if kind in ("av", "au", "az", "bq", "bz"):
                if kind[0] == "a":
                    c0 = {"au": 0, "av": 2048, "az": 4096}[kind] + i * 512
                    src = a_w_in[l][:, c0:c0 + 512]
                else:
                    c0 = {"bq": 0, "bz": 1024}[kind] + i * 512
                    src = b_w_in[l][:, c0:c0 + 512]
                if kind == "az":
                    zsrc = a_w_in[l][:, 4096:6144].rearrange("(k p) (oc i j) -> p k oc i j", p=128, oc=4, i=4)
                    zdstv = wv3.rearrange("p (k oc j) -> p k oc j", k=8, oc=4)
                    for oc in range(4):
                        dsts.append(zdstv[:, :, oc, :])
                        srcs.append(zsrc[:, :, oc, i, :])
                else:
                    dsts.append(wv3.rearrange("p (k n) -> p k n", k=8))
                    srcs.append(src.rearrange("(k p) n -> p k n", p=128))
            elif kind in ("ao", "bo"):
                wsrc = a_w_out[l] if kind == "ao" else b_w_out[l]
                dsts.append(wv3.rearrange("p (c d) -> p c d", c=4))
                srcs.append(wsrc[i * 512:(i + 1) * 512, :].rearrange("(c p) d -> p c d", p=128))
            elif kind == "wk":
                d5 = wv3.rearrange("p (k h u e) -> p k h u e", k=8, h=4, u=2)
                for hh in range(4):
                    for u in range(2):
                        dsts.append(d5[:, :, hh, u, :])
                        srcs.append(w_k[:, hh * 64:(hh + 1) * 64].rearrange("(k p) e -> p k e", p=128))
            elif kind == "wv":
                dsts.append(wv3[:, 0:2048].rearrange("p (k n) -> p k n", k=8))
                srcs.append(w_v.rearrange("(k p) n -> p k n", p=128))
            for dst, src in zip(dsts, srcs):
                key = f"cast{ci % 8}"
                ci += 1
                P.op("pool", (I("dma_start", out=dst, in_=src)),
                     w=[("wbf", si), ("castslot", key)], dma=key)

        def setup_load(dst, src, wkeys, slow=False, key="setup"):
            P.op("sp", (I("dma_start", out=dst, in_=src, allow_slow_non_contiguous=slow)),
                 w=list(wkeys) + [("setupq", key)], dma=key)

        setup_load(tmpb[1][:, 0:128], c_ident, [("stg", 0)])
        P.op("dve", I("tensor_copy", out=ident[:], in_=tmpb[1][:, 0:128]), r=[("stg", 0)], w=[("c", "ident")])
        P.op("pool", I("memset", ones[:], 1.0), w=[("c", "ones")])
        P.op("pool", I("memset", mhalf[:], -0.5), w=[("c", "mhalf")])
        P.op("pool", I("memset", epst[:], EPS), w=[("c", "epst")])
        P.op("pool", I("memset", stat[:], 0.0), w=[("st", i) for i in range(256)])
        P.op("pool", I("memset", Vaug[:], 1.0), w=[("V", s) for s in range(8)])
        P.op("pool", I("memset", sel[:], 0.0), w=[("c", "sel0")])
        P.op("pool", I("memset", sel[0:1, 0, 64:128], 1.0), r=[("c", "sel0")], w=[("c", "sel1")])
        P.op("pool", I("memset", sel[0:1, 1, 0:64], 1.0), r=[("c", "sel1")], w=[("c", "sel")])
        for hf in range(4):
            setup_load(tmpb[0][:, :], c_etab[:, hf * 1024:(hf + 1) * 1024], [("stg", 3)])
            P.op("dve", (I("tensor_copy",
                out=etab[:, hf * 2:(hf + 1) * 2, :].rearrange("p a b -> p (a b)"), in_=tmpb[0][:, :])),
                r=[("stg", 3)], w=[("c", "etab", hf)])
        setup_load(lng[:, :].rearrange("p (l c) -> p l c", l=2),
                   a_ln_g.rearrange("l (c p) -> p l c", p=128), [("c", "lng")], slow=True)
        setup_load(lnb[:, :].rearrange("p (l c) -> p l c", l=2),
                   a_ln_b.rearrange("l (c p) -> p l c", p=128), [("c", "lnb")], slow=True)
        setup_load(sink32[:], b_sinks, [("sink32",)])
        P.op("act", I("activation", out=sink32[:], in_=sink32[:], func=AF.Exp),
             r=[("sink32",)], w=[("sinkexp",)])
        for l in range(2):
            for k in range(4):
                for r_ in range(2):
                    idx = l * 8 + k * 2 + r_
                    for g2 in range(2):
                        c0 = l * 16 + 4 * k + 2 * g2 + r_
                        src = sink32[0:1, c0:c0 + 1].broadcast_to([1, 128])
                        P.op("dve", (I("tensor_copy",
                            out=esrow[0:1, idx, g2 * 128:(g2 + 1) * 128], in_=src)),
                            r=[("sinkexp",)], w=[("c", "esrow", idx, g2)])
        setup_load(tmpb[1][:, 128:256], c_mask, [("stg", 1)])
        stgU = bufU[:, :, :].rearrange("p c t -> p (c t)").bitcast(F32)
        for l in range(2):
            for g in range(8):
                lg = l * 8 + g
                setup_load(stgU[:, lg * 128:(lg + 1) * 128], a_w_s[l, g], [("stgw", lg)], key=f"su{lg % 4}")
                setup_load(stgU[:, 2048 + lg * 128:2048 + (lg + 1) * 128], a_b_s[l, g, :].partition_broadcast(128),
                           [("stgb", lg)], key=f"su{lg % 4}")
        for l in range(2):
            for g in range(8):
                lg = l * 8 + g
                xi = lg % 4
                P.op("dve", I("tensor_tensor", out=xs[xi][:, 0:128], in0=stgU[:, lg * 128:(lg + 1) * 128],
                              in1=tmpb[1][:, 128:256], op=ALU.mult),
                     r=[("stgw", lg), ("stg", 1)], w=[("xs", xi)])
                bk = newbank()
                P.op("pe", I("transpose", banks[bk][:, :].bitcast(BF16)[:, 0:128], xs[xi][:, 0:128], ident[:]),
                     r=[("xs", xi), ("c", "ident")], w=[("bank", bk)])
                P.op("act", I("copy", out=WcT[:, lg, :], in_=banks[bk][:, :].bitcast(BF16)[:, 0:128]),
                     r=[("bank", bk)], w=[("c", "WcT", lg)])
                bk2 = newbank()
                P.op("pe", I("matmul", banks[bk2][:, 0:128], lhsT=ones[:], rhs=WcT[:, lg, :], start=True, stop=True),
                     r=[("c", "ones"), ("c", "WcT", lg)], w=[("bank", bk2)])
                for sub in range(2):
                    cc = l * 16 + g * 2 + sub
                    P.op("dve", I("scalar_tensor_tensor", out=bias2[:, cc, :], in0=banks[bk2][:, 0:128],
                                  scalar=lnb[:, cc:cc + 1], in1=stgU[:, 2048 + lg * 128:2048 + (lg + 1) * 128],
                                  op0=ALU.mult, op1=ALU.add),
                         r=[("bank", bk2), ("stgb", lg), ("c", "lnb")], w=[("c", "bias2", cc)])
        for k_ in list(P.lw.keys()):
            if k_[0] == "c":
                P.const.add(k_)
        P.barrier()

        ringst = {"next_fill": 0, "next_get": 0, "retired": set(), "total": nslab * ntiles}

        def slab_words(g):
            kind = plan[g % nslab][0]
            return 2048 if kind == "wv" else SLABW

        def try_fill():
            while ringst["next_fill"] < ringst["total"]:
                g = ringst["next_fill"]
                if g >= NSLOT and (g - NSLOT) not in ringst["retired"]:
                    break
                slot = g % NSLOT
                si = g % nslab
                n = slab_words(g)
                P.op("sp", (I("dma_start", out=ring[slot][:, 0:n], in_=wbf[si][:, 0:n])),
                     r=[("wbf", si)], w=[("ring", slot)], dma=f"ring{slot}")
                ringst["next_fill"] += 1

        def ring_get(expect):
            g = ringst["next_get"]
            assert plan[g % nslab] == expect, (plan[g % nslab], expect)
            ringst["next_get"] += 1
            assert g < ringst["next_fill"], "ring deadlock"
            return g, g % NSLOT

        def ring_retire(g):
            ringst["retired"].add(g)
            try_fill()

        def load_gain(buf, key, src_row):
            P.op("sp", I("dma_start", out=buf[:], in_=src_row.partition_broadcast(128)), w=[(key,)], dma=key)

        def rstd_from(ss_col, n):
            c1 = newcol()
            c2 = newcol()
            P.op("dve", I("tensor_scalar", out=stat[:, c1:c1 + 1], in0=stat[:, ss_col:ss_col + 1],
                          scalar1=1.0 / n, scalar2=EPS, op0=ALU.mult, op1=ALU.add), r=stk(ss_col), w=stk(c1))
            P.op("pool", I("tensor_tensor", out=stat[:, c2:c2 + 1], in0=stat[:, c1:c1 + 1], in1=mhalf[:], op=ALU.pow),
                 r=stk(c1), w=stk(c2))
            return c2

        rstd_col = {}
        early_cols = {}

        def prenormA(b, gt, gkey, reuse_rstd=False, src=None, skeys=None):
            if src is None:
                src, skeys = h[:, b, :], [("h", b)]
            if not reuse_rstd:
                cs = newcol()
                P.op("act", I("activation", out=junk[:, 0:D], in_=src, func=AF.Square, accum_out=stat[:, cs:cs + 1]),
                     r=skeys, w=stk(cs) + [("junk", 0), ("junk", 1)])
                rstd_col[b] = rstd_from(cs, D)
            cr = rstd_col[b]
            P.op("dve", I("scalar_tensor_tensor", out=xs[b][:], in0=src, scalar=stat[:, cr:cr + 1], in1=gt[:],
                          op0=ALU.mult, op1=ALU.mult), r=skeys + [(gkey,)] + stk(cr), w=[("xs", b)])

        def prenormT(b, dstT, dkeys):
            bk = newbank()
            for kc in range(8):
                P.op("pe", I("transpose", banks[bk][:, :].bitcast(BF16)[:, kc * 128:(kc + 1) * 128],
                             xs[b][:, kc * 128:(kc + 1) * 128], ident[:]), r=[("xs", b)], w=[("bank", bk)])
            P.op("act", I("copy", out=dstT[:, :, b * 128:(b + 1) * 128],
                          in_=banks[bk][:, :].bitcast(BF16).rearrange("p (k t) -> p k t", k=8)),
                 r=[("bank", bk)], w=dkeys)

        def postnorm(b, bkA, bkB):
            y2 = psum[:, bkA:bkA + 2, :].rearrange("p a b -> p (a b)")
            cs = newcol()
            P.op("act", I("activation", out=junk[:, 1024:2048], in_=y2, func=AF.Square, accum_out=stat[:, cs:cs + 1]),
                 r=[("bank", bkA), ("bank", bkB)], w=stk(cs) + [("junk", 2), ("junk", 3)])
            ti = rot("tmp", 2)
            P.op("dve", I("tensor_tensor", out=tmpb[ti][:], in0=y2, in1=gpost[:], op=ALU.mult),
                 r=[("bank", bkA), ("bank", bkB), ("gpost",)] + stk(cs), w=[("tmp", ti, 0), ("tmp", ti, 1)])
            cr = rstd_from(cs, D)
            P.op("dve", I("scalar_tensor_tensor", out=h[:, b, :], in0=tmpb[ti][:], scalar=stat[:, cr:cr + 1], in1=h[:, b, :],
                          op0=ALU.mult, op1=ALU.add),
                 r=[("h", b), ("tmp", ti, 0), ("tmp", ti, 1)] + stk(cr), w=[("h", b)])

        Ukeys = lambda cc: [("U", cc, b) for b in range(NB)]

        def fm_proj(slot, oc, dst_fn, blocks=(0, 1, 2, 3), bk=None):
            if bk is None:
                bk = newbank()
            c0, c1 = blocks[0] * 128, (blocks[-1] + 1) * 128
            for kc in range(8):
                P.op("pe", I("matmul", banks[bk][:, c0:c1],
                             lhsT=ring[slot][:, kc * 512 + oc * 128:kc * 512 + (oc + 1) * 128],
                             rhs=xnT[:, kc, c0:c1], start=(kc == 0), stop=(kc == 7)),
                     r=[("ring", slot)] + [("xn", b) for b in blocks], w=[("bank", bk)])
            if dst_fn is not None:
                dst_fn(bk)
            return bk

        class Wout:
            def __init__(self, kind, l, ncc, ybase, after_block, load_next_gains, fixed_banks=None):
                self.kind, self.l, self.ncc, self.ybase = kind, l, ncc, ybase
                self.after_block, self.load_next_gains, self.fixed = after_block, load_next_gains, fixed_banks
                self.bk = {}

            def start(self):
                self.slabs = [ring_get((self.kind, self.l, i)) for i in range(self.ncc // 4)]

            def mm(self, b):
                bkA, bkB = self.fixed if self.fixed is not None else newbank2()
                self.bk[b] = (bkA, bkB)
                for hf, bk in enumerate((bkA, bkB)):
                    for cc in range(self.ncc):
                        g, slot = self.slabs[cc // 4]
                        P.op("pe", I("matmul", banks[bk][:, :], lhsT=bufU[:, self.ybase + cc, b * 128:(b + 1) * 128],
                                     rhs=ring[slot][:, (cc % 4) * 1024 + hf * 512:(cc % 4) * 1024 + (hf + 1) * 512],
                                     start=(cc == 0), stop=(cc == self.ncc - 1)),
                             r=[("ring", slot), ("U", self.ybase + cc, b)], w=[("bank", bk)])
                if b == NB - 1:
                    for g, slot in self.slabs:
                        ring_retire(g)

            def post(self, b):
                postnorm(b, *self.bk[b])

            def after(self, b):
                self.after_block(b)

            def finish(self):
                self.load_next_gains()

            def run_all(self):
                self.start()
                for b in range(NB):
                    self.mm(b)
                    self.post(b)
                    if b >= 1:
                        self.after(b - 1)
                self.after(NB - 1)
                self.finish()

        def wout_phase(kind, l, ncc, ybase, after_block, load_next_gains):
            Wout(kind, l, ncc, ybase, after_block, load_next_gains).run_all()

        def v_mm(slot, nb, b, vsum0):
            bk = newbank()
            for kc in range(8):
                P.op("pe", I("matmul", banks[bk][:, :], lhsT=xnT[:, kc, b * 128:(b + 1) * 128],
                             rhs=ring[slot][:, kc * 512:(kc + 1) * 512], start=(kc == 0), stop=(kc == 7)),
                     r=[("ring", slot), ("xn", b)], w=[("bank", bk)])
            cc_ = vsum0 + b * 4 + nb
            P.op("act", I("activation", out=vbuf[:, b, nb * 512:(nb + 1) * 512], in_=banks[bk][:, :],
                          func=AF.Gelu_apprx_tanh, accum_out=stat[:, cc_:cc_ + 1]),
                 r=[("bank", bk)], w=[("VZ", b * 4 + nb)] + stk(cc_))

        def a_layer(l, next_gain):
            load_gain(gpre, "gpre", next_gain)
            xk = lambda b: [("xn", b)]
            for b in range(3):
                prenormT(b, xnT, xk(b))
            vsum0 = newcol(16)
            g0, s0 = ring_get(("av", l, 0))
            g1, s1 = ring_get(("av", l, 1))
            for nb, slot in ((0, s0), (1, s1)):
                for b in range(3):
                    v_mm(slot, nb, b, vsum0)
            prenormT(3, xnT, xk(3))
            for nb, slot in ((0, s0), (1, s1)):
                v_mm(slot, nb, 3, vsum0)
            ring_retire(g0)
            ring_retire(g1)
            for nb in (2, 3):
                g, slot = ring_get(("av", l, nb))
                for b in range(NB):
                    v_mm(slot, nb, b, vsum0)
                ring_retire(g)
            for b in range(NB):
                vk = [("VZ", b * 4 + i) for i in range(4)]
                cq = newcol()
                P.op("act", I("activation", out=junk[:, :], in_=vbuf[:, b, :], func=AF.Square, accum_out=stat[:, cq:cq + 1]),
                     r=vk, w=stk(cq) + [("junk", i) for i in range(4)])
                c1, c2, c3, c4, c5 = [newcol() for _ in range(5)]
                s0_ = vsum0 + b * 4
                P.op("dve", I("tensor_reduce", out=stat[:, c1:c1 + 1], in_=stat[:, s0_:s0_ + 4], axis=AX.X, op=ALU.add),
                     r=stk(s0_, 4), w=stk(c1))
                P.op("dve", I("tensor_scalar", out=stat[:, c2:c2 + 1], in0=stat[:, c1:c1 + 1], scalar1=1.0 / 2048,
                              scalar2=None, op0=ALU.mult), r=stk(c1), w=stk(c2))
                P.op("dve", I("scalar_tensor_tensor", out=stat[:, c3:c3 + 1], in0=stat[:, c2:c2 + 1],
                              scalar=stat[:, c2:c2 + 1], in1=epst[:], op0=ALU.mult, op1=ALU.subtract),
                     r=stk(c2), w=stk(c3))
                P.op("dve", I("scalar_tensor_tensor", out=stat[:, c4:c4 + 1], in0=stat[:, cq:cq + 1], scalar=1.0 / 2048,
                              in1=stat[:, c3:c3 + 1], op0=ALU.mult, op1=ALU.subtract),
                     r=stk(cq) + stk(c3), w=stk(c4))
                P.op("pool", I("tensor_tensor", out=stat[:, c5:c5 + 1], in0=stat[:, c4:c4 + 1], in1=mhalf[:], op=ALU.pow),
                     r=stk(c4), w=stk(c5))
                P.op("dve", I("tensor_scalar", out=vbuf[:, b, :], in0=vbuf[:, b, :], scalar1=stat[:, c2:c2 + 1],
                              scalar2=stat[:, c5:c5 + 1], op0=ALU.subtract, op1=ALU.mult),
                     r=vk + stk(c2) + stk(c5), w=vk)
            def spatial(cc):
                bk = newbank()
                lg = l * 8 + cc // 2
                for b in range(NB):
                    P.op("pe", I("matmul", banks[bk][:, b * 128:(b + 1) * 128], lhsT=vbuf[:, b, cc * 128:(cc + 1) * 128],
                                 rhs=WcT[:, lg, :], start=True, stop=True),
                         r=[("VZ", b * 4 + cc // 4)], w=[("bank", bk)])
                ti = rot("t1", 2)
                lc = l * 16 + cc
                P.op("dve", I("scalar_tensor_tensor", out=t1b[ti][:, :].rearrange("p (b t) -> p b t", b=NB),
                              in0=banks[bk][:, :].rearrange("p (b t) -> p b t", b=NB), scalar=lng[:, lc:lc + 1],
                              in1=bias2[:, lc, :].unsqueeze(1).broadcast_to([128, NB, 128]), op0=ALU.mult, op1=ALU.add),
                     r=[("bank", bk)], w=[("t1", ti)])
                P.op("dve", I("tensor_tensor", out=bufU[:, cc, :], in0=t1b[ti][:, :], in1=bufU[:, cc, :], op=ALU.mult),
                     r=[("t1", ti)] + Ukeys(cc), w=Ukeys(cc))

            def udst(bk, cc):
                P.op("act", I("activation", out=bufU[:, cc, :], in_=banks[bk][:, :], func=AF.Gelu_apprx_tanh),
                     r=[("bank", bk)], w=Ukeys(cc))

            def zdst(bk, cc):
                P.op("act", I("activation", out=zT[:, cc, :], in_=banks[bk][:, :], func=AF.Silu),
                     r=[("bank", bk)], w=[("VZ", cc)])
                P.op("pool", I("tensor_tensor", out=bufU[:, cc, :], in0=bufU[:, cc, :], in1=zT[:, cc, :], op=ALU.mult),
                     r=[("VZ", cc)] + Ukeys(cc), w=Ukeys(cc))

            for nb in range(4):
                g, slot = ring_get(("au", l, nb))
                for oc in range(4):
                    fm_proj(slot, oc, (lambda bk, cc=nb * 4 + oc: udst(bk, cc)))
                    if nb == 3:
                        spatial(oc)
                ring_retire(g)
            for q in range(4):
                g, slot = ring_get(("az", l, q))
                for oc in range(4):
                    fm_proj(slot, oc, (lambda bk, cc=oc * 4 + q: zdst(bk, cc)))
                    if q < 3:
                        spatial(4 * (q + 1) + oc)
                ring_retire(g)

        def kv_phase(ti_):
            base = (4 * ti_) % 8
            xkvT = zT[:, 8:16, :]
            kvk = [("VZ", 8 + i) for i in range(8)]
            for b in range(NB):
                prenormA(b, gkv, "gkv", reuse_rstd=True)
            for b in range(NB):
                prenormT(b, xkvT, kvk)
            gk, sk = ring_get(("wk", 0, 0))
            for k in range(4):
                bk = newbank()
                for kc in range(8):
                    P.op("pe", I("matmul", banks[bk][:, :], lhsT=ring[sk][:, kc * 512 + k * 128:kc * 512 + (k + 1) * 128],
                                 rhs=xkvT[:, kc, :], start=(kc == 0), stop=(kc == 7)),
                         r=[("ring", sk)] + kvk, w=[("bank", bk)])
                P.op("act", I("copy", out=kT2[:, k, base * 128:(base + 4) * 128], in_=banks[bk][:, :]),
                     r=[("bank", bk)], w=[("kT", k, (base + i) % 8) for i in range(4)])
            ring_retire(gk)
            gv, sv = ring_get(("wv", 0, 0))
            for b in range(NB):
                bk = newbank()
                for kc in range(8):
                    P.op("pe", I("matmul", banks[bk][:, 0:256], lhsT=xkvT[:, kc, b * 128:(b + 1) * 128],
                                 rhs=ring[sv][:, kc * 256:(kc + 1) * 256], start=(kc == 0), stop=(kc == 7)),
                         r=[("ring", sv)] + kvk, w=[("bank", bk)])
                P.op("dve", I("tensor_copy", out=Vaug[:, base + b, :, 64:128],
                              in_=banks[bk][:, 0:256].rearrange("p (k e) -> p k e", k=4)),
                     r=[("bank", bk)], w=[("V", base + b)])
            ring_retire(gv)

        def b_layer(j, ti_, first_of_seq, next_gain, wo, mid_hook=None):
            base = (4 * ti_) % 8
            xk = lambda b: [("xn", b)]
            for b in range(3):
                prenormT(b, xnT, xk(b))
            q5 = bufU[:, 0:8, :].rearrange("p c t -> p (c t)").rearrange("p (k b g t) -> p k b g t", k=4, b=NB, g=2)
            qdst = lambda bk, cc: P.op("dve", I("tensor_copy", out=q5[:, cc // 2, :, cc % 2, :],
                                                in_=banks[bk][:, :].rearrange("p (b t) -> p b t", b=NB)),
                                       r=[("bank", bk)], w=Ukeys(cc))
            g, slot = ring_get(("bq", j, 0))
            bks = [fm_proj(slot, oc, None, blocks=(0, 1, 2)) for oc in range(4)]
            prenormT(3, xnT, xk(3))
            load_gain(gpre, "gpre", next_gain)
            for oc in range(4):
                fm_proj(slot, oc, (lambda bk, cc=oc: qdst(bk, cc)), blocks=(3,), bk=bks[oc])
            ring_retire(g)
            hookc = [0]

            def hook():
                if mid_hook is not None:
                    mid_hook(hookc[0])
                hookc[0] += 1

            g, slot = ring_get(("bq", j, 1))
            for oc in range(4):
                fm_proj(slot, oc, (lambda bk, cc=4 + oc: qdst(bk, cc)))
                hook()
            ring_retire(g)
            for nb in range(2):
                g, slot = ring_get(("bz", j, nb))
                for oc in range(4):
                    cc = nb * 4 + oc
                    fm_proj(slot, oc, (lambda bk, cc=cc: P.op("act", I("activation", out=zT[:, cc, :], in_=banks[bk][:, :],
                                                                       func=AF.Silu), r=[("bank", bk)], w=[("VZ", cc)])))
                    hook()
                ring_retire(g)
            if j == 0:
                kv_phase(ti_)
            steps = [(b, k) for b in range(NB) for k in range(4)]
            info = {}
            LOOK = 2

            def qk_stage(i):
                b, k = steps[i]
                js = [1] if (first_of_seq and b == 0) else [0, 1]
                slots = {0: (base + b - 1) % 8, 1: (base + b) % 8}
                lo = js[0] * 256
                pidx = {}
                for r_ in range(2):
                    bk = newbank_att()
                    for jj in js:
                        c0 = jj * 256
                        P.op("pe", I("matmul", banks[bk][:, c0:c0 + 256],
                                     lhsT=kT2[r_ * 64:(r_ + 1) * 64, k, slots[jj] * 128:(slots[jj] + 1) * 128],
                                     rhs=q5[r_ * 64:(r_ + 1) * 64, k, b, :, :].rearrange("p g t -> p (g t)"),
                                     start=True, stop=True),
                             r=[("kT", k, slots[jj]), ("U", 2 * k, b), ("U", 2 * k + 1, b)], w=[("bank", bk)])
                    pe_i = rot("pexp", 4)
                    P.op("act", I("activation", out=pexp[pe_i][:, lo:512], in_=banks[bk][:, lo:512], func=AF.Exp, scale=0.125),
                         r=[("bank", bk)], w=[("pexp", pe_i)])
                    pt_i = rot("pT", 8)
                    pidx[r_] = pt_i
                    P.op("pool", I("tensor_tensor", out=pT[pt_i][:, lo:512], in0=pexp[pe_i][:, lo:512],
                                   in1=etab[:, k * 2 + r_, lo:512], op=ALU.mult),
                         r=[("pexp", pe_i)], w=[("pT", pt_i)])
                info[i] = (js, slots, pidx)

            def pv_stage(i):
                b, k = steps[i]
                js, slots, pidx = info.pop(i)
                bk2 = newbank_att()
                for r_ in range(2):
                    vlo = 64 if r_ == 0 else 0
                    for jj in js:
                        P.op("pe", I("matmul", banks[bk2][:, r_ * 256:(r_ + 1) * 256], lhsT=Vaug[:, slots[jj], k, vlo:vlo + 128],
                                     rhs=pT[pidx[r_]][:, jj * 256:(jj + 1) * 256], start=(jj == js[0]), stop=False),
                             r=[("V", slots[jj]), ("pT", pidx[r_])], w=[("bank", bk2)])
                    ei = j * 8 + k * 2 + r_
                    P.op("pe", I("matmul", banks[bk2][:, r_ * 256:(r_ + 1) * 256], lhsT=sel[0:1, r_, :], rhs=esrow[0:1, ei, :],
                                 start=False, stop=True), w=[("bank", bk2)])
                li, ri, zi = rot("lnd", 2), rot("rden", 2), rot("rz", 2)
                for r_ in range(2):
                    plo, dlo = (0, 64) if r_ == 0 else (64, 0)
                    cs = slice(r_ * 256, (r_ + 1) * 256)
                    P.op("act", I("activation", out=lnd[li][plo:plo + 64, :], in_=banks[bk2][dlo:dlo + 64, cs], func=AF.Ln),
                         r=[("bank", bk2)], w=[("lnd", li, r_)])
                    P.op("act", I("activation", out=rden[ri][plo:plo + 64, :], in_=lnd[li][plo:plo + 64, :], func=AF.Exp,
                                  scale=-1.0), r=[("lnd", li, r_)], w=[("rden", ri, r_)])
                    P.op("dve", I("tensor_tensor", out=rz[zi][plo:plo + 64, :].rearrange("p (g t) -> p g t", g=2),
                                  in0=rden[ri][plo:plo + 64, :].rearrange("p (g t) -> p g t", g=2),
                                  in1=zT[plo:plo + 64, 2 * k:2 * k + 2, b * 128:(b + 1) * 128], op=ALU.mult),
                         r=[("rden", ri, r_), ("VZ", 2 * k), ("VZ", 2 * k + 1)], w=[("rz", zi, r_)])
                    P.op("dve", I("tensor_tensor", out=bufU[plo:plo + 64, 8 + 2 * k:8 + 2 * k + 2, b * 128:(b + 1) * 128],
                                  in0=banks[bk2][plo:plo + 64, cs].rearrange("p (g t) -> p g t", g=2),
                                  in1=rz[zi][plo:plo + 64, :].rearrange("p (g t) -> p g t", g=2), op=ALU.mult),
                         r=[("bank", bk2), ("rz", zi, r_)], w=[("U", 8 + 2 * k, b), ("U", 8 + 2 * k + 1, b)])

            wo.start()
            sched = {}
            for b in range(3 if INTERLEAVE_WOUT else 0):
                m_b = 4 * b + 3 + LOOK + 2
                sched.setdefault(m_b, []).append(("mm", b))
                sched.setdefault(m_b + 1, []).append(("post", b))
                sched.setdefault(m_b + 3, []).append(("after", b))
            done = set()
            nloop = len(steps) + LOOK
            for i in range(nloop):
                if i < len(steps):
                    qk_stage(i)
                if i >= LOOK:
                    pv_stage(i - LOOK)
                for what, b in sched.get(i, ()):
                    getattr(wo, what)(b)
                    done.add((what, b))
            for b in range(NB):
                for what in ("mm", "post"):
                    if (what, b) not in done:
                        getattr(wo, what)(b)
                if b >= 1 and ("after", b - 1) not in done:
                    wo.after(b - 1)
                    done.add(("after", b - 1))
            if ("after", NB - 1) not in done:
                wo.after(NB - 1)
            wo.finish()

        try_fill()
        load_gain(gpre, "gpre", a_pre_g[0, :])
        load_gain(gpost, "gpost", a_post_g[0, :])
        for b in range(NB):
            P.op("sp", I("dma_start", out=h[:, b, :], in_=x[0, b * 128:(b + 1) * 128, :]), w=[("h", b)], dma=f"hl{b}")
            prenormA(b, gpre, "gpre")
        for ti_ in range(ntiles):
            sq, tl = divmod(ti_, tiles_per_seq)
            t0 = tl * TT
            pre = lambda b: prenormA(b, gpre, "gpre")

            def tile_end(b, ti_=ti_, sq=sq, t0=t0):
                P.op("pool", I("dma_start", out=out[sq, t0 + b * 128:t0 + (b + 1) * 128, :], in_=h[:, b, :]),
                     r=[("h", b)], dma=f"hs{b}")
                if ti_ + 1 < ntiles:
                    sq2, tl2 = divmod(ti_ + 1, tiles_per_seq)
                    P.op("sp", I("dma_start", out=h[:, b, :], in_=x[sq2, tl2 * TT + b * 128:tl2 * TT + (b + 1) * 128, :]),
                         w=[("h", b)], dma=f"hl{b}")

            def early_prenorm(stage, ti_=ti_):
                if ti_ + 1 >= ntiles:
                    return
                sq2, tl2 = divmod(ti_ + 1, tiles_per_seq)
                vzf = bufVZ[:, 4096:8192].bitcast(F32)
                land = [(tmpb[0][:], [("tmp", 0, 0), ("tmp", 0, 1)], "xe0"), (tmpb[1][:], [("tmp", 1, 0), ("tmp", 1, 1)], "xe1"),
                        (vzf[:, 0:1024], [("VZ", 8 + i) for i in range(4)], "xe2"),
                        (vzf[:, 1024:2048], [("VZ", 12 + i) for i in range(4)], "xe3")]
                for b in range(NB):
                    dst, keys, dk = land[b]
                    if stage == 0:
                        P.op("sp", I("dma_start", out=dst, in_=x[sq2, tl2 * TT + b * 128:tl2 * TT + (b + 1) * 128, :]),
                             w=keys, dma=dk)
                    if stage == 2 + b:
                        cs = newcol()
                        early_cols[b] = cs
                        P.op("act", I("activation", out=junk[:, 0:D], in_=dst, func=AF.Square, accum_out=stat[:, cs:cs + 1]),
                             r=keys, w=stk(cs) + [("junk", 0), ("junk", 1)])
                    if stage == 4 + b:
                        rstd_col[b] = rstd_from(early_cols[b], D)
                    if stage == 6 + b:
                        prenormA(b, gpre, "gpre", reuse_rstd=True, src=dst, skeys=keys)

            a_layer(0, a_pre_g[1, :])
            wout_phase("ao", 0, 16, 0, pre, lambda: load_gain(gpost, "gpost", a_post_g[1, :]))
            load_gain(gkv, "gkv", kv_norm_g[0, :])
            a_layer(1, b_pre_g[0, :])
            wout_phase("ao", 1, 16, 0, pre, lambda: load_gain(gpost, "gpost", b_post_g[0, :]))
            b_layer(0, ti_, tl == 0, b_pre_g[1, :],
                    Wout("bo", 0, 8, 8, pre, lambda: load_gain(gpost, "gpost", b_post_g[1, :]), fixed_banks=(6, 7) if INTERLEAVE_WOUT else None))
            b_layer(1, ti_, tl == 0, a_pre_g[0, :],
                    Wout("bo", 1, 8, 8, tile_end, lambda: load_gain(gpost, "gpost", a_post_g[0, :]), fixed_banks=(6, 7) if INTERLEAVE_WOUT else None),
                    mid_hook=early_prenorm)
        fin = []
        for b in range(NB):
            fin.append((f"hs{b}", P.dma_cnt[f"hs{b}"] * 16))

        P.finalize()
        block = es.enter_context(nc.Block())

        @block.tensor
        def _(e):
            P.emit_engine("pe", e, esem, dsem)

        @block.scalar
        def _(e):
            P.emit_engine("act", e, esem, dsem)

        @block.vector
        def _(e):
            P.emit_engine("dve", e, esem, dsem)

        @block.gpsimd
        def _(e):
            P.emit_engine("pool", e, esem, dsem)
            for k_, v_ in fin:
                e.wait_ge(dsem[k_], v_)

        @block.sync
        def _(e):
            P.emit_engine("sp", e, esem, dsem)
    return nc


def make_consts():
    np.seterr(over='ignore')
    ident = np.eye(128, dtype=np.float32)
    mask = np.tril(np.ones((128, 128), dtype=np.float32))
    s = np.arange(128, dtype=np.float32)[:, None]
    t = np.arange(128, dtype=np.float32)[None, :]
    etab = np.zeros((128, 8, 2, 2, 128), dtype=np.float32)
    for k in range(4):
        for r in range(2):
            for g2 in range(2):
                hd = 4 * k + 2 * g2 + r
                slope = np.float32(2.0 ** (-8.0 * (hd + 1) / 16))
                d_prev = (128.0 + t - s).astype(np.float32)
                d_cur = (t - s).astype(np.float32)
                etab[:, k * 2 + r, 0, g2, :] = np.where(s > t, np.exp(-slope * d_prev), 0.0)
                etab[:, k * 2 + r, 1, g2, :] = np.where(s <= t, np.exp(-slope * d_cur), 0.0)
    return ident, mask, etab.reshape(128, 8 * 512).astype(np.float32)


_NC_CACHE = {}


def run(inputs, ncores, nseq, seqlen):
    key = (nseq, seqlen)
    if key not in _NC_CACHE:
        _NC_CACHE[key] = build_nc(nseq, seqlen)
    nc = _NC_CACHE[key]
    ident, mask, etab = make_consts()
    x = np.ascontiguousarray(inputs["x"], dtype=np.float32)
    shared = {k: np.ascontiguousarray(inputs[k], dtype=np.float32) for k in
              ("a_w_in", "a_ln_g", "a_ln_b", "a_w_s", "a_b_s", "a_w_out", "a_pre_g", "a_post_g",
               "w_k", "w_v", "b_w_in", "b_w_out", "b_pre_g", "b_post_g")}
    shared["kv_norm_g"] = np.ascontiguousarray(inputs["kv_norm_g"], dtype=np.float32).reshape(1, D)
    shared["b_sinks"] = np.ascontiguousarray(inputs["b_sinks"], dtype=np.float32).reshape(1, 32)
    shared["c_ident"], shared["c_mask"], shared["c_etab"] = ident, mask, etab
    in_maps = []
    for c in range(ncores):
        m = dict(shared)
        m["x"] = x[c * nseq:(c + 1) * nseq]
        in_maps.append(m)
    res = run_bass_kernel_spmd(nc, in_maps, core_ids=list(range(ncores)))
    return np.concatenate([r["out"] for r in res.results], axis=0)


def kernel(**inputs):
    x = inputs["x"]
    bsz, seqlen, _ = x.shape
    nseq = bsz // NCORES
    return run(inputs, NCORES, nseq, seqlen).astype(np.float32)
```
